# Optimizing a Trainium2 kernel written in Bass

```python
import math
import jax, jax.numpy as jnp
from jax import lax
import numpy as np

D_MODEL = 2048
BATCH = 4
SEQ = 2048
DEPTH = 4

ATTN_WIDTH = D_MODEL // 2
CONV_WIDTH = D_MODEL - ATTN_WIDTH
N_HEADS = 8
HEAD_DV = ATTN_WIDTH // N_HEADS
HEAD_DK = HEAD_DV // 2
N_CONV_GROUPS = 8
CONV_K = 3
D_FF = 4 * D_MODEL
NUM_BUCKETS = 32
MAX_DISTANCE = 128
MAX_EXACT = NUM_BUCKETS // 2
Q_BLOCK = 128
EPS = 1e-6
NEG = -1e30
PROJ_WIDTH = 3 * ATTN_WIDTH + 3 * CONV_WIDTH

kernel_name = "hybrid_diffattn_shortconv_sqrelu"


def _rmsnorm(x, g):
    x32 = x.astype(jnp.float32)
    y = x32 * lax.rsqrt(jnp.mean(x32 * x32, axis=-1, keepdims=True) + EPS)
    return (y * g.astype(jnp.float32)).astype(x.dtype)


def _relative_bucket(dist):
    n = jnp.maximum(dist, 0)
    is_small = n < MAX_EXACT
    nf = jnp.maximum(n, MAX_EXACT).astype(jnp.float32)
    large = MAX_EXACT + (jnp.log(nf / MAX_EXACT) / math.log(MAX_DISTANCE / MAX_EXACT)
                         * (NUM_BUCKETS - MAX_EXACT)).astype(jnp.int32)
    large = jnp.minimum(large, NUM_BUCKETS - 1)
    return jnp.where(is_small, n, large)


def _diff_attention(q, k, v, lam, rel_bias):
    B, S, H, _, dk = q.shape
    dv = v.shape[-1]
    nblk = S // Q_BLOCK
    k_pos = jnp.arange(S, dtype=jnp.int32)
    scale = dk ** -0.5

    def block(args):
        qb, start = args
        q_pos = start + jnp.arange(Q_BLOCK, dtype=jnp.int32)
        dist = q_pos[:, None] - k_pos[None, :]
        bias = jnp.take(rel_bias, _relative_bucket(dist), axis=0)
        bias = bias.reshape(Q_BLOCK, S, H, 2).transpose(2, 3, 0, 1).astype(jnp.float32)
        s = jnp.einsum('bqhmd,bkhmd->bhmqk', qb, k).astype(jnp.float32) * scale + bias
        s = jnp.where(dist >= 0, s, NEG)
        p = jax.nn.softmax(s, axis=-1)
        p = p[:, :, 0] - lam * p[:, :, 1]
        return jnp.einsum('bhqk,bkhd->bqhd', p.astype(v.dtype), v)

    qb = q.reshape(B, nblk, Q_BLOCK, H, 2, dk).transpose(1, 0, 2, 3, 4, 5)
    starts = jnp.arange(nblk, dtype=jnp.int32) * Q_BLOCK
    out = lax.map(block, (qb, starts))
    return out.transpose(1, 0, 2, 3, 4).reshape(B, S, H, dv)


def _short_conv(u, w):
    S = u.shape[1]
    up = jnp.pad(u, ((0, 0), (CONV_K - 1, 0), (0, 0)))
    return sum(w[i] * up[:, i:i + S] for i in range(CONV_K))


def setup_inputs(seed: int = 0) -> dict:
    key = jax.random.key(seed)
    ks = jax.random.split(key, 16)
    f32 = jnp.float32

    def nrm(k, shape, scale):
        return jax.random.normal(k, shape, f32) * scale

    def gain(k, shape):
        return 1.0 + 0.02 * jax.random.normal(k, shape, f32)

    return {
        "x": nrm(ks[0], (BATCH, SEQ, D_MODEL), 1.0),
        "w_in": nrm(ks[1], (DEPTH, D_MODEL, PROJ_WIDTH), D_MODEL ** -0.5),
        "w_out": nrm(ks[2], (DEPTH, ATTN_WIDTH + CONV_WIDTH, D_MODEL), (ATTN_WIDTH + CONV_WIDTH) ** -0.5),
        "conv_w": nrm(ks[3], (DEPTH, CONV_K, CONV_WIDTH), CONV_K ** -0.5),
        "q_norm_g": gain(ks[4], (DEPTH, HEAD_DK)),
        "k_norm_g": gain(ks[5], (DEPTH, HEAD_DK)),
        "lambda_q1": nrm(ks[6], (DEPTH, HEAD_DK), 0.1),
        "lambda_k1": nrm(ks[7], (DEPTH, HEAD_DK), 0.1),
        "lambda_q2": nrm(ks[8], (DEPTH, HEAD_DK), 0.1),
        "lambda_k2": nrm(ks[9], (DEPTH, HEAD_DK), 0.1),
        "subln_g": gain(ks[10], (DEPTH, HEAD_DV)),
        "attn_norm_g": gain(ks[11], (DEPTH, D_MODEL)),
        "mlp_norm_g": gain(ks[12], (DEPTH, D_MODEL)),
        "w_up": nrm(ks[13], (DEPTH, D_MODEL, D_FF), D_MODEL ** -0.5),
        "w_down": nrm(ks[14], (DEPTH, D_FF, D_MODEL), D_FF ** -0.5),
        "rel_bias": nrm(ks[15], (NUM_BUCKETS, 2 * N_HEADS), 0.5),
    }


def reference(x, w_in, w_out, conv_w, q_norm_g, k_norm_g, lambda_q1, lambda_k1,
              lambda_q2, lambda_k2, subln_g, attn_norm_g, mlp_norm_g, w_up, w_down,
              rel_bias):
    B, S, _ = x.shape
    splits = [ATTN_WIDTH, 2 * ATTN_WIDTH, 3 * ATTN_WIDTH,
              3 * ATTN_WIDTH + CONV_WIDTH, 3 * ATTN_WIDTH + 2 * CONV_WIDTH]
    for l in range(DEPTH):
        h = _rmsnorm(x, attn_norm_g[l])
        proj = h @ w_in[l]
        q, k, v, gate_b, gate_c, conv_in = jnp.split(proj, splits, axis=-1)

        q = _rmsnorm(q.reshape(B, S, N_HEADS, 2, HEAD_DK), q_norm_g[l])
        k = _rmsnorm(k.reshape(B, S, N_HEADS, 2, HEAD_DK), k_norm_g[l])
        v = v.reshape(B, S, N_HEADS, HEAD_DV)
        lam_init = 0.8 - 0.6 * math.exp(-0.3 * l)
        lam = (jnp.exp(jnp.sum(lambda_q1[l].astype(jnp.float32) * lambda_k1[l].astype(jnp.float32)))
               - jnp.exp(jnp.sum(lambda_q2[l].astype(jnp.float32) * lambda_k2[l].astype(jnp.float32)))
               + lam_init)
        attn = _diff_attention(q, k, v, lam, rel_bias)
        attn = (_rmsnorm(attn, subln_g[l]) * (1.0 - lam_init)).reshape(B, S, ATTN_WIDTH)

        conv = gate_b * _short_conv(gate_c * conv_in, conv_w[l])

        x = x + jnp.concatenate([attn, conv.astype(attn.dtype)], axis=-1) @ w_out[l]

        hm = _rmsnorm(x, mlp_norm_g[l]) @ w_up[l]
        x = x + jnp.square(jax.nn.relu(hm)) @ w_down[l]
    return x
```

```python
import math
import os
from contextlib import ExitStack

import numpy as np
import concourse.bass as bass
import concourse.mybir as mybir
from concourse.bass_utils import run_bass_kernel_spmd

F32 = mybir.dt.float32
BF16 = mybir.dt.bfloat16
AF = mybir.ActivationFunctionType
ALU = mybir.AluOpType
AX = mybir.AxisListType

D = 2048
S = 2048
NB = 4
DEPTH = 4
T = 1024
NCH = 16
H = 8
DFF = 8192
EPS = 1e-6
NEG = -1e30
WCOLS = 256
NWBUF = 4
KVW = 8 * 2048 + 32
USE_POW = True


class PseudoSem:
    def __init__(self, name, sem):
        self.name = name
        self.sem = sem
        self.count = 0


class Engine(PseudoSem):
    def __init__(self, name, handle_name, sem):
        super().__init__(name, sem)
        self.handle_name = handle_name
        self.seen = {}
        self.prog = []


class Buf:
    __slots__ = ("name", "last_w", "readers")

    def __init__(self, name):
        self.name = name
        self.last_w = None
        self.readers = {}


class FW:
    def __init__(self, nc, stack):
        self.nc = nc
        self.stack = stack
        self.engs = {}
        self.dsems = []
        for nm, hn in (("pe", "tensor"), ("act", "scalar"), ("dve", "vector"),
                       ("pool", "gpsimd"), ("sp", "sync")):
            sem = stack.enter_context(nc.semaphore("s_" + nm))
            self.engs[nm] = Engine(nm, hn, sem)

    def dma_sem(self, name):
        sem = self.stack.enter_context(self.nc.semaphore("d_" + name))
        ps = PseudoSem("d_" + name, sem)
        self.dsems.append(ps)
        return ps

    def drain(self, engname):
        eng = self.engs[engname]
        waits = [(p.sem, p.count) for p in self.dsems if p.count > 0]
        waits += [(e.sem, e.count) for e in self.engs.values() if e.count > 0 and e is not eng]

        def emit(h):
            for s, v in waits:
                h.wait_ge(s, v)
        eng.prog.append(emit)

    def _deps(self, eng, reads, writes):
        deps = {}

        def add(p):
            ps, n = p
            cur = deps.get(ps.name)
            if cur is None or cur[1] < n:
                deps[ps.name] = (ps, n)
        for b in reads:
            if b.last_w is not None:
                add(b.last_w)
        for b in writes:
            if b.last_w is not None:
                add(b.last_w)
            for r in b.readers.values():
                add(r)
        waits = []
        for ps, n in deps.values():
            if eng.seen.get(ps.name, 0) < n:
                eng.seen[ps.name] = n
                waits.append((ps.sem, n))
        return waits

    def op(self, engname, fn, reads=(), writes=(), signal=True):
        eng = self.engs[engname]
        waits = self._deps(eng, reads, writes)
        if signal:
            eng.count += 1
            n = eng.count
        else:
            n = eng.count + 1
        sem = eng.sem

        def emit(h):
            for s, v in waits:
                h.wait_ge(s, v)
            if signal:
                fn(h).then_inc(sem, 1)
            else:
                fn(h)
        eng.prog.append(emit)
        for b in reads:
            b.readers[eng.name] = (eng, n)
        for b in writes:
            b.last_w = (eng, n)
            b.readers = {}

    def dma(self, qname, fn, dsem, reads=(), writes=(), inc=16):
        eng = self.engs[qname]
        waits = self._deps(eng, reads, writes)
        dsem.count += inc
        n = dsem.count
        sem = dsem.sem

        def emit(h):
            for s, v in waits:
                h.wait_ge(s, v)
            fn(h).then_inc(sem, inc)
        eng.prog.append(emit)
        for b in reads:
            b.readers[dsem.name] = (dsem, n)
        for b in writes:
            b.last_w = (dsem, n)
            b.readers = {}

    def dma_batch(self, qname, dsem, items):
        eng = self.engs[qname]
        n_final = dsem.count + 16 * len(items)
        sem = dsem.sem
        for fn, reads, writes in items:
            waits = self._deps(eng, reads, writes)

            def emit(h, waits=waits, fn=fn):
                for s, v in waits:
                    h.wait_ge(s, v)
                fn(h).then_inc(sem, 16)
            eng.prog.append(emit)
        dsem.count = n_final
        for fn, reads, writes in items:
            for b in reads:
                b.readers[dsem.name] = (dsem, n_final)
            for b in writes:
                b.last_w = (dsem, n_final)
                b.readers = {}

    def wait_all(self, engname, bufs):
        eng = self.engs[engname]
        waits = self._deps(eng, bufs, bufs)

        def emit(h):
            for s, v in waits:
                h.wait_ge(s, v)
        eng.prog.append(emit)

    def finish(self):
        with self.nc.Block() as block:
            for e in self.engs.values():
                def body(h, e=e):
                    for f in e.prog:
                        f(h)
                getattr(block, e.handle_name)(body)


class Rot:
    def __init__(self, items):
        self.items = items
        self.i = 0

    def next(self):
        it = self.items[self.i % len(self.items)]
        self.i += 1
        return it


class _Stop(Exception):
    pass


def build_program(layers, stop=None):
    nc = bass.Bass("TRN2", target_bir_lowering=False)

    def din(name, shape, dt=F32):
        return nc.dram_tensor(name, list(shape), dt, kind="ExternalInput").ap()

    xT_d = din("xT", [D, T])
    w_in_d = {l: din(f"w_in{l}", [D, 6144]) for l in layers}
    w_out_d = {l: din(f"w_out{l}", [D, D]) for l in layers}
    w_up_d = {l: din(f"w_up{l}", [D, DFF]) for l in layers}
    w_down_d = {l: din(f"w_down{l}", [DFF, D]) for l in layers}
    gA_d = din("gA", [128, DEPTH, NCH])
    gM_d = din("gM", [128, DEPTH, NCH])
    gq_d = din("gq", [128, DEPTH])
    gk_d = din("gk", [128, DEPTH])
    gs_d = din("gs", [128, DEPTH])
    cw_d = din("cw", [128, DEPTH, 3, 8])
    lam_d = din("lamv", [128, 4, DEPTH, 64])
    crel_d = din("crel", [128, 16])
    t0_d = din("t0", [128, 16, 128])
    t1_d = din("t1", [128, 16, 128])
    maskc_d = din("maskc", [128, 1])
    hflag_d = din("hflag", [128, 1])
    ident_d = din("ident", [128, 128])
    bones_d = din("bones", [128, 128])
    yT_d = nc.dram_tensor("yT", [D, T], F32, kind="ExternalOutput").ap()
    kvinK = nc.dram_tensor("kvinK", [128, 8192], BF16)
    kvoutK = nc.dram_tensor("kvoutK", [256, 8192], BF16)
    kvinV = nc.dram_tensor("kvinV", [128, 8192], BF16)
    kvoutV = nc.dram_tensor("kvoutV", [256, 8192], BF16)
    kvinH = nc.dram_tensor("kvinH", [128, 512], BF16)
    kvoutH = nc.dram_tensor("kvoutH", [256, 512], BF16)
    tb_d = nc.dram_tensor("tb_d", [128, 8192], BF16)

    with ExitStack() as st:
        fw = FW(nc, st)

        def sb(name, shape, dt):
            return st.enter_context(nc.sbuf_tensor(name, list(shape), dt))

        xT = sb("xT_sb", [128, NCH, T], F32)
        R1 = sb("R1", [128, 16384], BF16)
        R2 = sb("R2", [128, 16384], BF16)
        Vt = sb("Vown", [128, 8192], BF16)
        wb = [sb(f"wb{i}", [128, NCH, WCOLS], BF16) for i in range(NWBUF)]
        t32 = [sb(f"t32_{i}", [128, 512], F32) for i in range(6)]
        tbf = [sb(f"tbf_{i}", [128, 512], BF16) for i in range(4)]
        onesb = sb("onesb", [128, 128], BF16)
        bonesb = sb("bonesb", [128, 128], BF16)
        identb = sb("identb", [128, 128], BF16)
        c32 = sb("c32", [128, 128], F32)
        c32b = sb("c32b", [128, 128], F32)
        gA = sb("gA_sb", [128, DEPTH, NCH], F32)
        gM = sb("gM_sb", [128, DEPTH, NCH], F32)
        gq = sb("gq_sb", [128, DEPTH], F32)
        gk = sb("gk_sb", [128, DEPTH], F32)
        gs = sb("gs_sb", [128, DEPTH], F32)
        cw = sb("cw_sb", [128, DEPTH, 3, 8], F32)
        lamv = sb("lamv_sb", [128, 4, DEPTH, 64], F32)
        lamp = sb("lamp_sb", [128, 2, DEPTH, 64], F32)
        lams = sb("lams_sb", [128, 2, DEPTH], F32)
        lame = sb("lame_sb", [128, 2, DEPTH], F32)
        nlam = sb("nlam_sb", [128, DEPTH], F32)
        crel = sb("crel_sb", [128, 16], F32)
        coth = sb("coth_sb", [128, 16], F32)
        maskc = sb("maskc_sb", [128, 1], F32)
        hflag = sb("hflag_sb", [128, 1], F32)
        gb01 = sb("gb01", [128, 8, 2], F32)
        uhs = sb("uhs", [128, 8, 2], F32)
        uhr = sb("uhr", [128, 8, 2], F32)
        dmix = sb("dmix", [128, 8, 2], BF16)
        dtmp = sb("dtmp", [128, 8, 2], F32)
        dtmp2 = sb("dtmp2", [128, 8, 2], F32)
        scr = sb("scr", [128, 2], F32)
        uhs_bf = sb("uhs_bf", [128, 512], BF16)
        uhr_bf = sb("uhr_bf", [128, 2, 16], BF16)
        uhs_t = sb("uhs_t", [128, 16], F32)
        epsb = {}
        for nm, val in (("d", D * EPS), ("qk", 64 * EPS), ("sl", 128 * EPS)):
            epsb[val] = sb("eps_" + nm, [128, 1], F32)

        ps = [st.enter_context(nc.psum_tensor(f"ps{i}", [128, 512], F32)) for i in range(8)]
        psB = [Buf(f"ps{i}") for i in range(8)]
        ps_all = Rot([(ps[i], psB[i]) for i in range(8)])
        ps_lo = Rot([(ps[i], psB[i]) for i in range(4)])

        hT = R1[:, :].rearrange("p (c t) -> p c t", t=T)
        tbv = R1[:, 0:8192].rearrange("p (k l m q) -> p k l m q", k=2, l=2, m=16)
        Pt = [R1[:, 8192 + i * 512: 8192 + (i + 1) * 512] for i in range(4)]
        kvo = [R1[:, 10240 + i * 2048: 10240 + (i + 1) * 2048] for i in range(2)]
        QT = R2[:, 0:8192].rearrange("p (h t) -> p h t", t=T)
        KT = R2[:, 8192:16384].rearrange("p (h t) -> p h t", t=T)
        aG = R2[:, :].rearrange("p (c t) -> p c t", t=T)
        cmix = QT
        Vv = Vt[:, :].rearrange("p (h j d) -> p h j d", h=8, j=8)
        gcu = R2[:, 8192:16384].bitcast(F32)[:, 0:2052].rearrange("p (c t) -> p c t", t=1026)
        acc = Vt[:, :].bitcast(F32)[:, 0:2048].rearrange("p (c t) -> p c t", t=T)

        xB = [Buf(f"x{c}") for c in range(NCH)]
        hB = [Buf(f"h{tt}") for tt in range(2)]
        wbB = [[Buf(f"wb{i}_{j}") for j in range(2)] for i in range(NWBUF)]
        wsem = [[fw.dma_sem(f"w{i}_{j}") for j in range(2)] for i in range(NWBUF)]
        t32R = Rot([(t32[i], Buf(f"t32_{i}")) for i in range(6)])
        tbfR = Rot([(tbf[i], Buf(f"tbf_{i}")) for i in range(4)])
        PtB = [Buf(f"P{i}") for i in range(4)]
        PtR = Rot([(Pt[i], PtB[i]) for i in range(4)])
        kvoB = [Buf(f"kvo{i}") for i in range(2)]
        kvosem = [fw.dma_sem(f"kvo{i}") for i in range(2)]
        tbB = Buf("tb")
        QB = [[Buf(f"Q{h}_{tt}") for tt in range(2)] for h in range(8)]
        KB = [Buf(f"K{h}") for h in range(8)]
        VB = [Buf(f"V{h}") for h in range(8)]
        aGB = [Buf(f"aG{tt}") for tt in range(2)]
        gcuB, accB = Buf("gcu"), Buf("acc")
        cmixB = [Buf(f"cmix{tt}") for tt in range(2)]
        constB = Buf("const")
        c32B = Buf("c32")
        gb01B, uhsB, uhrB, dmixB, dtmpB = Buf("gb01"), Buf("uhs"), Buf("uhr"), Buf("dmix"), Buf("dtmp")
        tbdB = Buf("tbd")
        kvinKB, kvoutKB, kvinVB, kvoutVB, kvinHB, kvoutHB = [Buf(n) for n in ("kvinK", "kvoutK", "kvinV", "kvoutV", "kvinH", "kvoutH")]
        kvsemK, kvsemV = fw.dma_sem("kvK"), fw.dma_sem("kvV")
        ccsemK, ccsemV, ccsemH = fw.dma_sem("ccK"), fw.dma_sem("ccV"), fw.dma_sem("ccH")
        csem = fw.dma_sem("c")
        tssem = fw.dma_sem("ts")
        uhsem = fw.dma_sem("uh")
        xsem = fw.dma_sem("x")
        kvsem = fw.dma_sem("kv")
        ccsem = fw.dma_sem("cc")
        tbsem = fw.dma_sem("tb")
        osem = fw.dma_sem("o")

        wq = []
        wstate = {"issued": 0, "used": 0}

        def w_issue():
            i = wstate["issued"]
            if i >= len(wq):
                return
            src, nk = wq[i]
            slot = i % NWBUF
            v = src.rearrange("(c p) n -> p c n", p=128)
            hk = nk // 2
            for j in range(2):
                fw.dma("pool", lambda h, slot=slot, j=j, v=v, hk=hk: h.dma_start(
                    out=wb[slot][:, j * hk:(j + 1) * hk, :], in_=v[:, j * hk:(j + 1) * hk, :]),
                    wsem[slot][j], writes=[wbB[slot][j]])
            wstate["issued"] += 1

        def w_next():
            i = wstate["used"]
            while wstate["issued"] < min(len(wq), i + NWBUF):
                w_issue()
            wstate["used"] += 1
            slot = i % NWBUF
            return wb[slot], wbB[slot]

        def plan_blocks():
            for l in layers:
                for s in range(4):
                    for base in (4096, 5120, 3072):
                        wq.append((w_in_d[l][:, base + s * WCOLS: base + (s + 1) * WCOLS], 16))
                for j in range(8):
                    wq.append((w_out_d[l][1024:2048, j * WCOLS:(j + 1) * WCOLS], 8))
                for j in range(4):
                    wq.append((w_in_d[l][:, 1024 + j * WCOLS: 1024 + (j + 1) * WCOLS], 16))
                for j in range(4):
                    wq.append((w_in_d[l][:, 2048 + j * WCOLS: 2048 + (j + 1) * WCOLS], 16))
                for j in range(4):
                    wq.append((w_in_d[l][:, j * WCOLS:(j + 1) * WCOLS], 16))
                for j in range(8):
                    wq.append((w_out_d[l][0:1024, j * WCOLS:(j + 1) * WCOLS], 8))
                for j in range(8):
                    wq.append((w_out_d[l][1024:2048, j * WCOLS:(j + 1) * WCOLS], 8))
                for g in range(4):
                    for j in range(8):
                        wq.append((w_up_d[l][:, g * 2048 + j * WCOLS: g * 2048 + (j + 1) * WCOLS], 16))
                    for j in range(8):
                        wq.append((w_down_d[l][g * 2048:(g + 1) * 2048, j * WCOLS:(j + 1) * WCOLS], 16))
        plan_blocks()

        def guard(bufs):
            fw.op("dve", lambda h: h.memset(scr[:], 0.0), writes=list(bufs))

        def mm(out, lhsT, rhs, start, stop, reads, writes, force_signal=False):
            fw.op("pe", lambda h: h.matmul(out, lhsT, rhs, start=start, stop=stop), reads=reads, writes=writes,
                  signal=bool(stop) or len(writes) > 0 or force_signal)

        def rs_from_ss(dst, dstB, src, srcB, addc):
            fw.op("act", lambda h: h.activation(dst, src, AF.Ln, bias=epsb[addc][:, 0:1], scale=1.0),
                  reads=[srcB, constB], writes=[dstB])
            fw.op("act", lambda h: h.activation(dst, dst, AF.Exp, scale=-0.5), reads=[dstB], writes=[dstB])

        citems = []

        def cload(dst, src):
            citems.append((lambda h: h.dma_start(out=dst, in_=src), [], [constB]))
        cload(gA[:], gA_d)
        cload(gM[:], gM_d)
        cload(gq[:], gq_d)
        cload(gk[:], gk_d)
        cload(gs[:], gs_d)
        cload(cw[:], cw_d)
        cload(lamv[:], lam_d)
        cload(crel[:], crel_d)
        cload(maskc[:], maskc_d)
        cload(hflag[:], hflag_d)
        cload(c32[:], ident_d)
        cload(c32b[:], bones_d)
        fw.dma_batch("sp", csem, citems)
        fw.dma_batch("sp", xsem, [(lambda h, c=c: h.dma_start(out=xT[:, c, :], in_=xT_d[c * 128:(c + 1) * 128, :]), [], [xB[c]])
                                  for c in range(NCH)])
        fw.op("dve", lambda h: h.tensor_copy(identb[:], c32[:]), reads=[constB], writes=[constB])
        fw.op("dve", lambda h: h.tensor_copy(bonesb[:], c32b[:]), reads=[constB], writes=[constB])
        fw.op("dve", lambda h: h.memset(onesb[:], 1.0), writes=[constB])
        fw.op("dve", lambda h: h.memset(uhs_bf[:], 0.0), writes=[uhsB])
        for val, tl in epsb.items():
            fw.op("dve", lambda h, val=val, tl=tl: h.memset(tl[:], float(val)), writes=[constB])
        fw.op("dve", lambda h: h.tensor_scalar_mul(gA[:], gA[:], float(math.sqrt(D))), reads=[constB], writes=[constB])
        fw.op("dve", lambda h: h.tensor_scalar_mul(gM[:], gM[:], float(math.sqrt(D))), reads=[constB], writes=[constB])
        fw.op("dve", lambda h: h.tensor_scalar_mul(gk[:], gk[:], 8.0), reads=[constB], writes=[constB])
        for l in range(DEPTH):
            lam_init = 0.8 - 0.6 * math.exp(-0.3 * l)
            fw.op("dve", lambda h, l=l, li=lam_init: h.tensor_scalar_mul(
                gs[:, l:l + 1], gs[:, l:l + 1], float((1.0 - li) * math.sqrt(128.0))), reads=[constB], writes=[constB])
        fw.op("dve", lambda h: h.tensor_tensor(lamp[:, 0], lamv[:, 0], lamv[:, 1], ALU.mult), reads=[constB], writes=[constB])
        fw.op("dve", lambda h: h.tensor_tensor(lamp[:, 1], lamv[:, 2], lamv[:, 3], ALU.mult), reads=[constB], writes=[constB])
        fw.op("dve", lambda h: h.reduce_sum(lams[:], lamp[:], AX.X), reads=[constB], writes=[constB])
        fw.op("act", lambda h: h.activation(lame[:], lams[:], AF.Exp), reads=[constB], writes=[constB])
        fw.op("dve", lambda h: h.tensor_tensor(nlam[:], lame[:, 1], lame[:, 0], ALU.subtract), reads=[constB], writes=[constB])
        for l in range(DEPTH):
            lam_init = 0.8 - 0.6 * math.exp(-0.3 * l)
            fw.op("dve", lambda h, l=l, li=lam_init: h.tensor_scalar_add(nlam[:, l:l + 1], nlam[:, l:l + 1], float(-li)),
                  reads=[constB], writes=[constB])
        fw.op("dve", lambda h: h.tensor_scalar(coth[:], crel[:], maskc[:, 0:1], None, ALU.add), reads=[constB], writes=[constB])
        tstage = R2[:, 0:8192].bitcast(F32)[:, 0:2048].rearrange("p (m q) -> p m q", q=128)
        tdiff = R2[:, 8192:16384].bitcast(F32)[:, 0:2048].rearrange("p (m q) -> p m q", q=128)
        thl = Vt[:, 0:4096].rearrange("p (l m q) -> p l m q", l=2, m=16)
        tstB = Buf("tstage")
        tbd_v = tb_d[:, :].rearrange("p (k l m q) -> p k l m q", k=2, l=2, m=16)
        for kind, src in ((0, t0_d), (1, t1_d)):
            fw.dma("sp", lambda h, src=src: h.dma_start(out=tstage, in_=src), tssem, writes=[tstB])
            for m in range(16):
                fw.op("dve", lambda h, m=m: h.tensor_scalar(tstage[:, m, :], tstage[:, m, :], crel[:, m:m + 1], None, ALU.subtract),
                      reads=[tstB, constB], writes=[tstB])
            fw.op("dve", lambda h: h.tensor_copy(thl[:, 0], tstage), reads=[tstB], writes=[tstB])
            fw.op("dve", lambda h: h.tensor_tensor(tdiff, tstage, thl[:, 0], ALU.subtract), reads=[tstB], writes=[tstB])
            fw.op("dve", lambda h: h.tensor_copy(thl[:, 1], tdiff), reads=[tstB], writes=[tstB])
            fw.dma("sp", lambda h, kind=kind: h.dma_start(out=tbd_v[:, kind], in_=thl), tbsem, reads=[tstB], writes=[tbdB])
        all_r2v = [cmixB[0], cmixB[1], gcuB, accB] + KB + VB + [b for q in QB for b in q] + aGB
        guard([tstB] + all_r2v)

        def rmsnorm_x(gt, l):
            guards = []
            for tt in range(2):
                sl = slice(tt * 512, (tt + 1) * 512)
                pst, pB = ps_all.next()
                for c in range(NCH):
                    sq, sqB = tbfR.next()
                    fw.op("act", lambda h, sq=sq, c=c, sl=sl: h.activation(sq[:], xT[:, c, sl], AF.Square),
                          reads=[xB[c]], writes=[sqB])
                    mm(pst[:], onesb[:], sq[:], c == 0, c == NCH - 1, [sqB, constB], [pB] if c in (0, NCH - 1) else [], force_signal=True)
                rs, rsB = t32R.next()
                rs_from_ss(rs[:], rsB, pst[:], pB, D * EPS)
                for c in range(NCH):
                    fw.op("dve", lambda h, c=c, sl=sl, rs=rs: h.scalar_tensor_tensor(
                        hT[:, c, sl], xT[:, c, sl], gt[:, l, c:c + 1], rs[:], ALU.mult, ALU.mult),
                        reads=[xB[c], rsB, constB] + guards, writes=[hB[tt]])

        def block_fm(nk, rhs_fn, rhs_bufs, evac_fn, ntiles=2):
            wt, wB = w_next()
            pend = []
            for nt in range(ntiles):
                for tt in range(2):
                    pst, pB = ps_all.next()
                    for c in range(nk):
                        mm(pst[:], wt[:, c, nt * 128:(nt + 1) * 128], rhs_fn(c, tt), c == 0, c == nk - 1,
                           [wB[0], wB[1]] + rhs_bufs(c, tt), [pB] if c in (0, nk - 1) else [])
                    if pend:
                        pend.pop()()
                    r = evac_fn(nt, tt, pst, pB)
                    if r is not None:
                        pend.append(r)
            if pend:
                pend.pop()()

        def layer(l):
            guards = []
            rmsnorm_x(gA, l)
            hr = lambda c, tt: hT[:, c, tt * 512:(tt + 1) * 512]
            hb = lambda c, tt: [hB[tt]]

            if stop == 'P1':
                raise _Stop()
            for s in range(4):
                fw.op("dve", lambda h: h.memset(gcu[:, :, 0:2], 0.0), reads=guards, writes=[gcuB])

                def ev_gc(nt, tt, pst, pB):
                    fw.op("act", lambda h: h.activation(gcu[:, nt, 2 + tt * 512: 2 + (tt + 1) * 512], pst[:], AF.Copy),
                          reads=[pB], writes=[gcuB])
                block_fm(16, hr, hb, ev_gc)
                if stop == 'P2a1':
                    raise _Stop()

                def ev_cin(nt, tt, pst, pB):
                    sl = slice(2 + tt * 512, 2 + (tt + 1) * 512)
                    fw.op("dve", lambda h: h.tensor_tensor(gcu[:, nt, sl], pst[:], gcu[:, nt, sl], ALU.mult),
                          reads=[pB, gcuB], writes=[gcuB])
                block_fm(16, hr, hb, ev_cin)
                if stop == 'P2a2':
                    raise _Stop()
                for ci in range(2):
                    ch = 2 * s + ci
                    fw.op("dve", lambda h, ci=ci, ch=ch: h.tensor_scalar(
                        acc[:, ci, :], gcu[:, ci, 2:1026], cw[:, l, 2, ch:ch + 1], None, ALU.mult),
                        reads=[gcuB, constB] + guards, writes=[accB])
                    fw.op("dve", lambda h, ci=ci, ch=ch: h.scalar_tensor_tensor(
                        acc[:, ci, :], gcu[:, ci, 1:1025], cw[:, l, 1, ch:ch + 1], acc[:, ci, :], ALU.mult, ALU.add),
                        reads=[gcuB, accB, constB], writes=[accB])
                    fw.op("dve", lambda h, ci=ci, ch=ch: h.scalar_tensor_tensor(
                        acc[:, ci, :], gcu[:, ci, 0:1024], cw[:, l, 0, ch:ch + 1], acc[:, ci, :], ALU.mult, ALU.add),
                        reads=[gcuB, accB, constB], writes=[accB])
                    fw.op("dve", lambda h, ci=ci, ch=ch: h.tensor_copy(uhs[:, ch, :], gcu[:, ci, 1024:1026]),
                          reads=[gcuB], writes=[uhsB])

                if stop == 'P2a3':
                    raise _Stop()

                def ev_gb(nt, tt, pst, pB):
                    ch = 2 * s + nt
                    if tt == 0 and os.environ.get("DBG_NOGB") != "1":
                        fw.op("act", lambda h: h.activation(gb01[:, ch, :], pst[:, 0:2], AF.Copy), reads=[pB], writes=[gb01B])
                    if os.environ.get("DBG_NOCMIX") == "1":
                        return None
                    fw.op("dve", lambda h: h.tensor_tensor(cmix[:, ch, tt * 512:(tt + 1) * 512], pst[:],
                                                           acc[:, nt, tt * 512:(tt + 1) * 512], ALU.mult),
                          reads=[pB, accB] + ([gb01B] if tt == 0 else []), writes=[cmixB[tt]])
                block_fm(16, hr, hb, ev_gb)
                if stop == 'P2a4':
                    raise _Stop()

            if stop == 'P2a':
                raise _Stop()
            def make_ev_x(jb):
                def ev(nt, tt, pst, pB):
                    c = 2 * jb + nt
                    sl = slice(tt * 512, (tt + 1) * 512)
                    fw.op("dve", lambda h: h.tensor_tensor(xT[:, c, sl], pst[:], xT[:, c, sl], ALU.add),
                          reads=[pB, xB[c]], writes=[xB[c]])
                return ev
            for jb in range(8):
                block_fm(8, lambda c, tt: cmix[:, c, tt * 512:(tt + 1) * 512], lambda c, tt: [cmixB[tt]], make_ev_x(jb))

            if stop == 'P2b':
                raise _Stop()
            def make_ev_qk(dst, dstB_fn, gcol):
                def ev(nt, tt, pst, pB):
                    sq, sqB = tbfR.next()
                    fw.op("act", lambda h: h.activation(sq[:], pst[:], AF.Square), reads=[pB], writes=[sqB])

                    def later():
                        p2, p2B = ps_all.next()
                        mm(p2[:], bonesb[:], sq[:], True, True, [sqB, constB], [p2B])
                        rs, rsB = t32R.next()
                        rs_from_ss(rs[:], rsB, p2[:], p2B, 64 * EPS)
                        fw.op("dve", lambda h: h.scalar_tensor_tensor(
                            dst(nt, tt), pst[:], gcol, rs[:], ALU.mult, ALU.mult),
                            reads=[pB, rsB, constB], writes=dstB_fn(nt, tt))
                    return later
                return ev
            for jb in range(4):
                block_fm(16, hr, hb, make_ev_qk(
                    lambda nt, tt, jb=jb: KT[:, 2 * jb + nt, tt * 512:(tt + 1) * 512],
                    lambda nt, tt, jb=jb: [KB[2 * jb + nt], gcuB], gk[:, l:l + 1]))
            for jb in range(4):
                wt, wB = w_next()
                for j in range(8):
                    pst, pB = ps_all.next()
                    for c in range(NCH):
                        mm(pst[:, 0:WCOLS], hT[:, c, j * 128:(j + 1) * 128], wt[:, c, :], c == 0, c == NCH - 1,
                           [wB[0], wB[1], hB[j // 4]], [pB] if c in (0, NCH - 1) else [])
                    fw.op("act", lambda h, j=j, jb=jb, pst=pst: h.activation(
                        Vv[:, 2 * jb:2 * jb + 2, j, :], pst[:, 0:WCOLS].rearrange("p (h d) -> p h d", h=2), AF.Copy),
                        reads=[pB], writes=[VB[2 * jb], VB[2 * jb + 1], accB])
            if stop == 'P2c':
                raise _Stop()
            uhs_f = uhs[:, :, :].rearrange("p c t -> p (c t)")
            fw.op("dve", lambda h: h.tensor_copy(uhs_bf[:, 0:16], uhs_f), reads=[uhsB], writes=[uhsB])
            fw.op("dve", lambda h: h.tensor_tensor(uhs_t[:], uhs_f, uhs_bf[:, 0:16], ALU.subtract), reads=[uhsB], writes=[uhsB])
            fw.op("dve", lambda h: h.tensor_copy(uhs_bf[:, 16:32], uhs_t[:]), reads=[uhsB], writes=[uhsB])
            groups = [[0, 1], [2, 3], [4, 5], [6, 7]]
            fw.dma("sp", lambda h: h.dma_start(out=kvinK[:, :].rearrange("p (h x) -> p h x", h=8), in_=KT), kvsemK, reads=KB, writes=[kvinKB])
            fw.dma("pool", lambda h: h.collective_compute("AllGather", ALU.bypass, replica_groups=groups,
                                                          ins=[kvinK.ap().opt()], outs=[kvoutK.ap().opt()]),
                   ccsemK, reads=[kvinKB], writes=[kvoutKB], inc=1)
            fw.dma_batch("sp", kvsemV, [
                (lambda h: h.dma_start(out=kvinV[:, :], in_=Vt[:, :]), VB, [kvinVB]),
                (lambda h: h.dma_start(out=kvinH[:, :], in_=uhs_bf[:, :]), [uhsB], [kvinHB]),
            ])
            fw.dma("pool", lambda h: h.collective_compute("AllGather", ALU.bypass, replica_groups=groups,
                                                          ins=[kvinV.ap().opt()], outs=[kvoutV.ap().opt()]),
                   ccsemV, reads=[kvinVB], writes=[kvoutVB], inc=1)
            fw.dma("pool", lambda h: h.collective_compute("AllGather", ALU.bypass, replica_groups=groups,
                                                          ins=[kvinH.ap().opt()], outs=[kvoutH.ap().opt()]),
                   ccsemH, reads=[kvinHB], writes=[kvoutHB], inc=1)
            if stop == 'EX':
                raise _Stop()
            for jb in range(4):
                block_fm(16, hr, hb, make_ev_qk(
                    lambda nt, tt, jb=jb: QT[:, 2 * jb + nt, tt * 512:(tt + 1) * 512],
                    lambda nt, tt, jb=jb: [QB[2 * jb + nt][tt], cmixB[tt]], gq[:, l:l + 1]))

            if stop == 'P2d':
                raise _Stop()
            fw.dma("sp", lambda h: h.dma_start(out=R1[:, 0:8192], in_=tb_d[:, :]), tbsem, reads=[tbdB], writes=[tbB, hB[0], hB[1]])
            fw.dma("sp", lambda h: h.dma_start(out=uhr_bf[:, :, :].rearrange("p a b -> p (a b)"), in_=kvoutH[0:128, 0:32]),
                   uhsem, reads=[kvoutHB], writes=[uhrB])
            fw.op("dve", lambda h: h.tensor_tensor(uhr[:, :, :].rearrange("p c t -> p (c t)"), uhr_bf[:, 0, :], uhr_bf[:, 1, :], ALU.add),
                  reads=[uhrB], writes=[uhrB])
            O1, O1B = ps[4], psB[4]
            O2, O2B = ps[5], psB[5]
            S1, S1B = ps[6], psB[6]
            S2, S2B = ps[7], psB[7]
            Oacc = [(O1, O1B, S1, S1B), (O2, O2B, S2, S2B)]

            def load_other(hh):
                i = hh % 2
                fw.dma_batch("sp", kvosem[i], [
                    (lambda h: h.dma_start(out=kvo[i][:, 0:1024], in_=kvoutK[0:128, hh * 1024:(hh + 1) * 1024]), [kvoutKB], [kvoBuf[i]]),
                    (lambda h: h.dma_start(out=kvo[i][:, 1024:2048], in_=kvoutV[0:128, hh * 1024:(hh + 1) * 1024]), [kvoutVB], [kvoBuf[i]]),
                ])
            kvoBuf = kvoB
            load_other(0)
            for hh in range(8):
                if hh + 1 < 8:
                    load_other(hh + 1)
                ko = kvo[hh % 2]
                koB = kvoB[hh % 2]
                for I in range(2):
                    qlo = I * 512
                    steps = [(False, j) for j in range(4 * I + 4)] + [(True, j) for j in range(8)]
                    nsteps = len(steps)
                    pend = None
                    for si, (oth, j) in enumerate(steps):
                        if oth:
                            q0 = qlo
                        else:
                            q0 = max(qlo, j * 128)
                        N = qlo + 512 - q0
                        c0 = q0 - qlo
                        Ptiles = []
                        for m in range(2):
                            hm = 2 * hh + m
                            pst, pB = ps_lo.next()
                            prow = slice(m * 64, (m + 1) * 64)
                            if oth:
                                lhsT = ko[prow, j * 128:(j + 1) * 128]
                                kb = koB
                            else:
                                lhsT = KT[prow, hh, j * 128:(j + 1) * 128]
                                kb = KB[hh]
                            tl = []
                            if oth:
                                if j == 7 and I == 0:
                                    tl.append((1, 0))
                            else:
                                if j * 128 >= qlo:
                                    tl.append((0, 0))
                                    if (j + 1) * 128 < qlo + 512:
                                        tl.append((1, 128))
                                elif (j + 1) * 128 == qlo:
                                    tl.append((1, 0))
                            mm(pst[:, 0:N], lhsT, QT[prow, hh, q0:q0 + N], True, len(tl) == 0,
                               [kb, QB[hh][I]], [pB])
                            for ti, (kind, off) in enumerate(tl):
                                for hl in range(2):
                                    last = (ti == len(tl) - 1) and hl == 1
                                    mm(pst[:, off:off + 128], identb[:], tbv[:, kind, hl, hm, :], False, last,
                                       [tbB, constB], [pB] if last else [])
                            pt, ptB = PtR.next()
                            bias_ap = coth[:, hm:hm + 1] if oth else crel[:, hm:hm + 1]
                            fw.op("act", lambda h, pt=pt, pst=pst, N=N, bias_ap=bias_ap: h.activation(
                                pt[:, 0:N], pst[:, 0:N], AF.Exp, bias=bias_ap, scale=1.0),
                                reads=[pB, constB], writes=[ptB])
                            Ptiles.append((pt, ptB, N, c0))
                        if pend is not None:
                            pend()

                        def pv(si=si, oth=oth, j=j, Ptiles=Ptiles):
                            first = si == 0
                            last = si == nsteps - 1
                            for m in range(2):
                                pt, ptB, N, c0 = Ptiles[m]
                                Oa, OaB, Sa, SaB = Oacc[m]
                                if oth:
                                    vl = ko[:, 1024 + j * 128: 1024 + (j + 1) * 128]
                                    vb = koB
                                else:
                                    vl = Vv[:, hh, j, :]
                                    vb = VB[hh]
                                mm(Oa[:, c0:c0 + N], vl, pt[:, 0:N], first, last, [ptB, vb], [OaB] if (first or last) else [])
                                mm(Sa[:, c0:c0 + N], onesb[:], pt[:, 0:N], first, last, [ptB, constB], [SaB] if (first or last) else [])
                        pend = pv
                    pend()
                    r1, r1B = t32R.next()
                    r2, r2B = t32R.next()
                    fw.op("dve", lambda h, r1=r1: h.reciprocal(r1[:], S1[:]), reads=[S1B], writes=[r1B])
                    fw.op("dve", lambda h, r2=r2: h.reciprocal(r2[:], S2[:]), reads=[S2B], writes=[r2B])
                    fw.op("dve", lambda h, r1=r1: h.tensor_tensor(r1[:], O1[:], r1[:], ALU.mult), reads=[O1B, r1B], writes=[r1B])
                    fw.op("dve", lambda h, r2=r2: h.tensor_tensor(r2[:], O2[:], r2[:], ALU.mult), reads=[O2B, r2B], writes=[r2B])
                    fw.op("dve", lambda h, r1=r1, r2=r2: h.scalar_tensor_tensor(
                        r1[:], r2[:], nlam[:, l:l + 1], r1[:], ALU.mult, ALU.add), reads=[r1B, r2B, constB], writes=[r1B])
                    sq, sqB = tbfR.next()
                    fw.op("dve", lambda h, r1=r1, sq=sq: h.tensor_tensor(sq[:], r1[:], r1[:], ALU.mult), reads=[r1B], writes=[sqB])
                    p2, p2B = ps_lo.next()
                    mm(p2[:], onesb[:], sq[:], True, True, [sqB, constB], [p2B])
                    rs_from_ss(r2[:], r2B, p2[:], p2B, 128 * EPS)
                    fw.op("dve", lambda h, r1=r1, r2=r2, hh=hh, I=I: h.scalar_tensor_tensor(
                        QT[:, hh, I * 512:(I + 1) * 512], r1[:], gs[:, l:l + 1], r2[:], ALU.mult, ALU.mult),
                        reads=[r1B, r2B, constB], writes=[QB[hh][I]])

            if stop == 'P3':
                raise _Stop()
            for jb in range(8):
                block_fm(8, lambda c, tt: QT[:, c, tt * 512:(tt + 1) * 512], lambda c, tt: [QB[c][tt]], make_ev_x(jb))
            fw.op("dve", lambda h: h.tensor_tensor(dtmp[:, :, 0], uhr[:, :, 0], cw[:, l, 0, :], ALU.mult), reads=[uhrB, constB], writes=[dtmpB])
            fw.op("dve", lambda h: h.tensor_tensor(dtmp2[:, :, 0], uhr[:, :, 1], cw[:, l, 1, :], ALU.mult), reads=[uhrB, constB, dtmpB], writes=[dtmpB])
            fw.op("dve", lambda h: h.tensor_tensor(dtmp[:, :, 0], dtmp[:, :, 0], dtmp2[:, :, 0], ALU.add), reads=[dtmpB], writes=[dtmpB])
            fw.op("dve", lambda h: h.tensor_tensor(dtmp[:, :, 1], uhr[:, :, 1], cw[:, l, 0, :], ALU.mult), reads=[uhrB, constB, dtmpB], writes=[dtmpB])
            fw.op("dve", lambda h: h.tensor_tensor(dtmp[:, :, :], dtmp[:, :, :], gb01[:, :, :], ALU.mult), reads=[dtmpB, gb01B], writes=[dtmpB])
            fw.op("dve", lambda h: h.tensor_scalar(dmix[:, :, :], dtmp[:, :, :], hflag[:, 0:1], None, ALU.mult), reads=[dtmpB, constB], writes=[dmixB])
            for jb in range(8):
                wt, wB = w_next()
                for nt in range(2):
                    c = 2 * jb + nt
                    pst, pB = ps_all.next()
                    for k in range(8):
                        mm(pst[:, 0:2], wt[:, k, nt * 128:(nt + 1) * 128], dmix[:, k, :], k == 0, k == 7,
                           [wB[0], wB[1], dmixB], [pB] if k in (0, 7) else [])
                    fw.op("dve", lambda h, c=c, pst=pst: h.tensor_tensor(xT[:, c, 0:2], pst[:, 0:2], xT[:, c, 0:2], ALU.add),
                          reads=[pB, xB[c]], writes=[xB[c]])

            if stop == 'P4':
                raise _Stop()
            guard([tbB] + PtB + kvoB + [hB[0], hB[1]])
            rmsnorm_x(gM, l)
            guard(all_r2v)
            for g in range(4):
                def make_ev_up(jb):
                    def ev(nt, tt, pst, pB):
                        cc = 2 * jb + nt
                        r, rB = tbfR.next()
                        fw.op("act", lambda h: h.activation(r[:], pst[:], AF.Relu), reads=[pB], writes=[rB])
                        fw.op("dve", lambda h: h.tensor_tensor(aG[:, cc, tt * 512:(tt + 1) * 512], r[:], r[:], ALU.mult),
                              reads=[rB], writes=[aGB[tt]])
                    return ev
                for jb in range(8):
                    block_fm(16, hr, hb, make_ev_up(jb))
                for jb in range(8):
                    block_fm(16, lambda c, tt: aG[:, c, tt * 512:(tt + 1) * 512], lambda c, tt: [aGB[tt]], make_ev_x(jb))
            guard(all_r2v)

        try:
            for l in layers:
                layer(l)
        except _Stop:
            pass

        outB = Buf("out")
        fw.dma_batch("sp", osem, [(lambda h, c=c: h.dma_start(out=yT_d[c * 128:(c + 1) * 128, :], in_=xT[:, c, :]), [xB[c]], [outB])
                                  for c in range(NCH)])
        fw.wait_all("sp", [outB])
        fw.drain("sp")
        fw.finish()
    return nc


def _bucket_table():
    n = np.arange(256)
    max_exact, nb, maxd = 16, 32, 128
    nf = np.maximum(n, max_exact).astype(np.float32)
    large = max_exact + (np.log(nf / max_exact) / math.log(maxd / max_exact) * (nb - max_exact)).astype(np.int32)
    large = np.minimum(large, nb - 1)
    return np.where(n < max_exact, n, large)


def _prep_common(inp):
    f = np.float32
    c = {}
    for k in ("w_in", "w_out", "w_up", "w_down"):
        for l in range(DEPTH):
            c[f"{k}{l}"] = np.ascontiguousarray(inp[k][l], dtype=f)
    c["gA"] = np.ascontiguousarray(inp["attn_norm_g"].reshape(DEPTH, NCH, 128).transpose(2, 0, 1), dtype=f)
    c["gM"] = np.ascontiguousarray(inp["mlp_norm_g"].reshape(DEPTH, NCH, 128).transpose(2, 0, 1), dtype=f)
    c["gq"] = np.ascontiguousarray(np.tile(inp["q_norm_g"].T, (2, 1)), dtype=f)
    c["gk"] = np.ascontiguousarray(np.tile(inp["k_norm_g"].T, (2, 1)), dtype=f)
    c["gs"] = np.ascontiguousarray(inp["subln_g"].T, dtype=f)
    c["cw"] = np.ascontiguousarray(inp["conv_w"].reshape(DEPTH, 3, 8, 128).transpose(3, 0, 1, 2), dtype=f)
    lam = np.stack([inp["lambda_q1"], inp["lambda_k1"], inp["lambda_q2"], inp["lambda_k2"]], axis=0)
    c["lamv"] = np.ascontiguousarray(np.broadcast_to(lam[None], (128, 4, DEPTH, 64)), dtype=f)
    rb = np.asarray(inp["rel_bias"], dtype=f)
    c["crel"] = np.ascontiguousarray(np.broadcast_to(rb[31][None, :], (128, 16)), dtype=f)
    bt = _bucket_table()
    kk = np.arange(128)[:, None]
    qq = np.arange(128)[None, :]
    d0 = qq - kk
    g0 = rb[bt[np.clip(d0, 0, 255)]]
    t0 = np.where((d0 >= 0)[:, :, None], g0, f(NEG))
    t1 = rb[bt[np.clip(128 + d0, 0, 255)]]
    c["t0"] = np.ascontiguousarray(t0.transpose(0, 2, 1), dtype=f)
    c["t1"] = np.ascontiguousarray(t1.transpose(0, 2, 1), dtype=f)
    c["ident"] = np.eye(128, dtype=f)
    bo = np.zeros((128, 128), dtype=f)
    bo[:64, :64] = 1.0
    bo[64:, 64:] = 1.0
    c["bones"] = bo
    return c


def _run(nc, common, xTs, n_cores=8, layers=tuple(range(DEPTH))):
    in_maps = []
    wkeys = [f"{k}{l}" for k in ("w_in", "w_out", "w_up", "w_down") for l in range(DEPTH)]
    for core in range(n_cores):
        half = core % 2
        m = {k: v for k, v in common.items() if k not in wkeys}
        for k in ("w_in", "w_out", "w_up", "w_down"):
            for l in layers:
                m[f"{k}{l}"] = common[f"{k}{l}"]
        m["xT"] = xTs[core]
        m["maskc"] = np.full((128, 1), 0.0 if half == 1 else NEG, dtype=np.float32)
        m["hflag"] = np.full((128, 1), 1.0 if half == 1 else 0.0, dtype=np.float32)
        in_maps.append(m)
    res = run_bass_kernel_spmd(nc, in_maps, core_ids=list(range(n_cores)))
    return [r["yT"] for r in res.results]


_PROG_CACHE = {}


def kernel(**inputs):
    inp = {k: np.asarray(v) for k, v in inputs.items()}
    x = np.asarray(inp["x"], dtype=np.float32)
    common = _prep_common(inp)
    xTs = []
    for core in range(8):
        b, half = core // 2, core % 2
        xTs.append(np.ascontiguousarray(x[b, half * T:(half + 1) * T, :].T))
    key = tuple(range(DEPTH))
    if key not in _PROG_CACHE:
        _PROG_CACHE[key] = build_program(list(range(DEPTH)))
    outs = _run(_PROG_CACHE[key], common, xTs)
    y = np.empty((NB, S, D), dtype=np.float32)
    for core in range(8):
        b, half = core // 2, core % 2
        y[b, half * T:(half + 1) * T, :] = outs[core].T
    return y
```

```python
import math
import os
from contextlib import ExitStack

import numpy as np
import concourse.bass as bass
import concourse.mybir as mybir
from concourse.bass_utils import run_bass_kernel_spmd

F32 = mybir.dt.float32
BF16 = mybir.dt.bfloat16
AF = mybir.ActivationFunctionType
ALU = mybir.AluOpType
AX = mybir.AxisListType

D = 2048
S = 2048
NB = 4
DEPTH = 4
T = 1024
NCH = 16
H = 8
DFF = 8192
EPS = 1e-6
NEG = -1e30
WCOLS = 256
NWBUF = 4
KVW = 8 * 2048 + 32
USE_POW = True


class PseudoSem:
    def __init__(self, name, sem):
        self.name = name
        self.sem = sem
        self.count = 0


class Engine(PseudoSem):
    def __init__(self, name, handle_name, sem):
        super().__init__(name, sem)
        self.handle_name = handle_name
        self.seen = {}
        self.prog = []


class Buf:
    __slots__ = ("name", "last_w", "readers")

    def __init__(self, name):
        self.name = name
        self.last_w = None
        self.readers = {}


class FW:
    def __init__(self, nc, stack):
        self.nc = nc
        self.stack = stack
        self.engs = {}
        self.dsems = []
        for nm, hn in (("pe", "tensor"), ("act", "scalar"), ("dve", "vector"),
                       ("pool", "gpsimd"), ("sp", "sync")):
            sem = stack.enter_context(nc.semaphore("s_" + nm))
            self.engs[nm] = Engine(nm, hn, sem)

    def dma_sem(self, name):
        sem = self.stack.enter_context(self.nc.semaphore("d_" + name))
        ps = PseudoSem("d_" + name, sem)
        self.dsems.append(ps)
        return ps

    def drain(self, engname):
        eng = self.engs[engname]
        waits = [(p.sem, p.count) for p in self.dsems if p.count > 0]
        waits += [(e.sem, e.count) for e in self.engs.values() if e.count > 0 and e is not eng]

        def emit(h):
            for s, v in waits:
                h.wait_ge(s, v)
        eng.prog.append(emit)

    def _deps(self, eng, reads, writes):
        deps = {}

        def add(p):
            ps, n = p
            cur = deps.get(ps.name)
            if cur is None or cur[1] < n:
                deps[ps.name] = (ps, n)
        for b in reads:
            if b.last_w is not None:
                add(b.last_w)
        for b in writes:
            if b.last_w is not None:
                add(b.last_w)
            for r in b.readers.values():
                add(r)
        waits = []
        for ps, n in deps.values():
            if eng.seen.get(ps.name, 0) < n:
                eng.seen[ps.name] = n
                waits.append((ps.sem, n))
        return waits

    def op(self, engname, fn, reads=(), writes=(), signal=True):
        eng = self.engs[engname]
        waits = self._deps(eng, reads, writes)
        if signal:
            eng.count += 1
            n = eng.count
        else:
            n = eng.count + 1
        sem = eng.sem

        def emit(h):
            for s, v in waits:
                h.wait_ge(s, v)
            if signal:
                fn(h).then_inc(sem, 1)
            else:
                fn(h)
        eng.prog.append(emit)
        for b in reads:
            b.readers[eng.name] = (eng, n)
        for b in writes:
            b.last_w = (eng, n)
            b.readers = {}

    def dma(self, qname, fn, dsem, reads=(), writes=(), inc=16):
        eng = self.engs[qname]
        waits = self._deps(eng, reads, writes)
        dsem.count += inc
        n = dsem.count
        sem = dsem.sem

        def emit(h):
            for s, v in waits:
                h.wait_ge(s, v)
            fn(h).then_inc(sem, inc)
        eng.prog.append(emit)
        for b in reads:
            b.readers[dsem.name] = (dsem, n)
        for b in writes:
            b.last_w = (dsem, n)
            b.readers = {}

    def dma_batch(self, qname, dsem, items):
        eng = self.engs[qname]
        n_final = dsem.count + 16 * len(items)
        sem = dsem.sem
        for fn, reads, writes in items:
            waits = self._deps(eng, reads, writes)

            def emit(h, waits=waits, fn=fn):
                for s, v in waits:
                    h.wait_ge(s, v)
                fn(h).then_inc(sem, 16)
            eng.prog.append(emit)
        dsem.count = n_final
        for fn, reads, writes in items:
            for b in reads:
                b.readers[dsem.name] = (dsem, n_final)
            for b in writes:
                b.last_w = (dsem, n_final)
                b.readers = {}

    def wait_all(self, engname, bufs):
        eng = self.engs[engname]
        waits = self._deps(eng, bufs, bufs)

        def emit(h):
            for s, v in waits:
                h.wait_ge(s, v)
        eng.prog.append(emit)

    def finish(self):
        with self.nc.Block() as block:
            for e in self.engs.values():
                def body(h, e=e):
                    for f in e.prog:
                        f(h)
                getattr(block, e.handle_name)(body)


class Rot:
    def __init__(self, items):
        self.items = items
        self.i = 0

    def next(self):
        it = self.items[self.i % len(self.items)]
        self.i += 1
        return it


class _Stop(Exception):
    pass


def build_program(layers, stop=None):
    nc = bass.Bass("TRN2", target_bir_lowering=False)

    def din(name, shape, dt=F32):
        return nc.dram_tensor(name, list(shape), dt, kind="ExternalInput").ap()

    xT_d = din("xT", [D, T])
    w_in_d = {l: din(f"w_in{l}", [D, 6144]) for l in layers}
    w_out_d = {l: din(f"w_out{l}", [D, D]) for l in layers}
    w_up_d = {l: din(f"w_up{l}", [D, DFF]) for l in layers}
    w_down_d = {l: din(f"w_down{l}", [DFF, D]) for l in layers}
    gA_d = din("gA", [128, DEPTH, NCH])
    gM_d = din("gM", [128, DEPTH, NCH])
    gq_d = din("gq", [128, DEPTH])
    gk_d = din("gk", [128, DEPTH])
    gs_d = din("gs", [128, DEPTH])
    cw_d = din("cw", [128, DEPTH, 3, 8])
    lam_d = din("lamv", [128, 4, DEPTH, 64])
    crel_d = din("crel", [128, 16])
    t0_d = din("t0", [128, 16, 128])
    t1_d = din("t1", [128, 16, 128])
    maskc_d = din("maskc", [128, 1])
    hflag_d = din("hflag", [128, 1])
    ident_d = din("ident", [128, 128])
    bones_d = din("bones", [128, 128])
    yT_d = nc.dram_tensor("yT", [D, T], F32, kind="ExternalOutput").ap()
    kvinK = nc.dram_tensor("kvinK", [128, 8192], BF16)
    kvoutK = nc.dram_tensor("kvoutK", [256, 8192], BF16)
    kvinV = nc.dram_tensor("kvinV", [128, 8192], BF16)
    kvoutV = nc.dram_tensor("kvoutV", [256, 8192], BF16)
    kvinH = nc.dram_tensor("kvinH", [128, 512], BF16)
    kvoutH = nc.dram_tensor("kvoutH", [256, 512], BF16)
    tb_d = nc.dram_tensor("tb_d", [128, 8192], BF16)

    with ExitStack() as st:
        fw = FW(nc, st)

        def sb(name, shape, dt):
            return st.enter_context(nc.sbuf_tensor(name, list(shape), dt))

        xT = sb("xT_sb", [128, NCH, T], F32)
        R1 = sb("R1", [128, 16384], BF16)
        R2 = sb("R2", [128, 16384], BF16)
        Vt = sb("Vown", [128, 8192], BF16)
        wb = [sb(f"wb{i}", [128, NCH, WCOLS], BF16) for i in range(NWBUF)]
        t32 = [sb(f"t32_{i}", [128, 512], F32) for i in range(6)]
        tbf = [sb(f"tbf_{i}", [128, 512], BF16) for i in range(4)]
        onesb = sb("onesb", [128, 128], BF16)
        bonesb = sb("bonesb", [128, 128], BF16)
        identb = sb("identb", [128, 128], BF16)
        c32 = sb("c32", [128, 128], F32)
        c32b = sb("c32b", [128, 128], F32)
        gA = sb("gA_sb", [128, DEPTH, NCH], F32)
        gM = sb("gM_sb", [128, DEPTH, NCH], F32)
        gq = sb("gq_sb", [128, DEPTH], F32)
        gk = sb("gk_sb", [128, DEPTH], F32)
        gs = sb("gs_sb", [128, DEPTH], F32)
        cw = sb("cw_sb", [128, DEPTH, 3, 8], F32)
        lamv = sb("lamv_sb", [128, 4, DEPTH, 64], F32)
        lamp = sb("lamp_sb", [128, 2, DEPTH, 64], F32)
        lams = sb("lams_sb", [128, 2, DEPTH], F32)
        lame = sb("lame_sb", [128, 2, DEPTH], F32)
        nlam = sb("nlam_sb", [128, DEPTH], F32)
        crel = sb("crel_sb", [128, 16], F32)
        coth = sb("coth_sb", [128, 16], F32)
        maskc = sb("maskc_sb", [128, 1], F32)
        hflag = sb("hflag_sb", [128, 1], F32)
        gb01 = sb("gb01", [128, 8, 2], F32)
        uhs = sb("uhs", [128, 8, 2], F32)
        uhr = sb("uhr", [128, 8, 2], F32)
        dmix = sb("dmix", [128, 8, 2], BF16)
        dtmp = sb("dtmp", [128, 8, 2], F32)
        dtmp2 = sb("dtmp2", [128, 8, 2], F32)
        scr = sb("scr", [128, 2], F32)
        uhs_bf = sb("uhs_bf", [128, 512], BF16)
        uhr_bf = sb("uhr_bf", [128, 2, 16], BF16)
        uhs_t = sb("uhs_t", [128, 16], F32)
        epsb = {}
        for nm, val in (("d", D * EPS), ("qk", 64 * EPS), ("sl", 128 * EPS)):
            epsb[val] = sb("eps_" + nm, [128, 1], F32)

        ps = [st.enter_context(nc.psum_tensor(f"ps{i}", [128, 512], F32)) for i in range(8)]
        psB = [Buf(f"ps{i}") for i in range(8)]
        ps_all = Rot([(ps[i], psB[i]) for i in range(8)])
        ps_lo = Rot([(ps[i], psB[i]) for i in range(4)])

        hT = R1[:, :].rearrange("p (c t) -> p c t", t=T)
        tbv = R1[:, 0:8192].rearrange("p (k l m q) -> p k l m q", k=2, l=2, m=16)
        Pt = [R1[:, 8192 + i * 512: 8192 + (i + 1) * 512] for i in range(4)]
        kvo = [R1[:, 10240 + i * 2048: 10240 + (i + 1) * 2048] for i in range(2)]
        QT = R2[:, 0:8192].rearrange("p (h t) -> p h t", t=T)
        KT = R2[:, 8192:16384].rearrange("p (h t) -> p h t", t=T)
        aG = R2[:, :].rearrange("p (c t) -> p c t", t=T)
        cmix = QT
        Vv = Vt[:, :].rearrange("p (h j d) -> p h j d", h=8, j=8)
        gcu = R2[:, 8192:16384].bitcast(F32)[:, 0:2052].rearrange("p (c t) -> p c t", t=1026)
        acc = Vt[:, :].bitcast(F32)[:, 0:2048].rearrange("p (c t) -> p c t", t=T)

        xB = [Buf(f"x{c}") for c in range(NCH)]
        hB = [Buf(f"h{tt}") for tt in range(2)]
        wbB = [[Buf(f"wb{i}_{j}") for j in range(2)] for i in range(NWBUF)]
        wsem = [[fw.dma_sem(f"w{i}_{j}") for j in range(2)] for i in range(NWBUF)]
        t32R = Rot([(t32[i], Buf(f"t32_{i}")) for i in range(6)])
        tbfR = Rot([(tbf[i], Buf(f"tbf_{i}")) for i in range(4)])
        PtB = [Buf(f"P{i}") for i in range(4)]
        PtR = Rot([(Pt[i], PtB[i]) for i in range(4)])
        kvoB = [Buf(f"kvo{i}") for i in range(2)]
        kvosem = [fw.dma_sem(f"kvo{i}") for i in range(2)]
        tbB = Buf("tb")
        QB = [[Buf(f"Q{h}_{tt}") for tt in range(2)] for h in range(8)]
        KB = [Buf(f"K{h}") for h in range(8)]
        VB = [Buf(f"V{h}") for h in range(8)]
        aGB = [Buf(f"aG{tt}") for tt in range(2)]
        gcuB, accB = Buf("gcu"), Buf("acc")
        cmixB = [Buf(f"cmix{tt}") for tt in range(2)]
        constB = Buf("const")
        c32B = Buf("c32")
        gb01B, uhsB, uhrB, dmixB, dtmpB = Buf("gb01"), Buf("uhs"), Buf("uhr"), Buf("dmix"), Buf("dtmp")
        tbdB = Buf("tbd")
        kvinKB, kvoutKB, kvinVB, kvoutVB, kvinHB, kvoutHB = [Buf(n) for n in ("kvinK", "kvoutK", "kvinV", "kvoutV", "kvinH", "kvoutH")]
        kvsemK, kvsemV = fw.dma_sem("kvK"), fw.dma_sem("kvV")
        ccsemK, ccsemV, ccsemH = fw.dma_sem("ccK"), fw.dma_sem("ccV"), fw.dma_sem("ccH")
        csem = fw.dma_sem("c")
        tssem = fw.dma_sem("ts")
        uhsem = fw.dma_sem("uh")
        xsem = fw.dma_sem("x")
        kvsem = fw.dma_sem("kv")
        ccsem = fw.dma_sem("cc")
        tbsem = fw.dma_sem("tb")
        osem = fw.dma_sem("o")

        wq = []
        wstate = {"issued": 0, "used": 0}

        def w_issue():
            i = wstate["issued"]
            if i >= len(wq):
                return
            src, nk = wq[i]
            slot = i % NWBUF
            v = src.rearrange("(c p) n -> p c n", p=128)
            hk = nk // 2
            for j in range(2):
                fw.dma("pool", lambda h, slot=slot, j=j, v=v, hk=hk: h.dma_start(
                    out=wb[slot][:, j * hk:(j + 1) * hk, :], in_=v[:, j * hk:(j + 1) * hk, :]),
                    wsem[slot][j], writes=[wbB[slot][j]])
            wstate["issued"] += 1

        def w_next():
            i = wstate["used"]
            while wstate["issued"] < min(len(wq), i + NWBUF):
                w_issue()
            wstate["used"] += 1
            slot = i % NWBUF
            return wb[slot], wbB[slot]

        def plan_blocks():
            for l in layers:
                for s in range(4):
                    for base in (4096, 5120, 3072):
                        wq.append((w_in_d[l][:, base + s * WCOLS: base + (s + 1) * WCOLS], 16))
                for j in range(8):
                    wq.append((w_out_d[l][1024:2048, j * WCOLS:(j + 1) * WCOLS], 8))
                for j in range(4):
                    wq.append((w_in_d[l][:, 1024 + j * WCOLS: 1024 + (j + 1) * WCOLS], 16))
                for j in range(4):
                    wq.append((w_in_d[l][:, 2048 + j * WCOLS: 2048 + (j + 1) * WCOLS], 16))
                for j in range(4):
                    wq.append((w_in_d[l][:, j * WCOLS:(j + 1) * WCOLS], 16))
                for j in range(8):
                    wq.append((w_out_d[l][0:1024, j * WCOLS:(j + 1) * WCOLS], 8))
                for j in range(8):
                    wq.append((w_out_d[l][1024:2048, j * WCOLS:(j + 1) * WCOLS], 8))
                for g in range(4):
                    for j in range(8):
                        wq.append((w_up_d[l][:, g * 2048 + j * WCOLS: g * 2048 + (j + 1) * WCOLS], 16))
                    for j in range(8):
                        wq.append((w_down_d[l][g * 2048:(g + 1) * 2048, j * WCOLS:(j + 1) * WCOLS], 16))
        plan_blocks()

        def guard(bufs):
            fw.op("dve", lambda h: h.memset(scr[:], 0.0), writes=list(bufs))

        def mm(out, lhsT, rhs, start, stop, reads, writes, force_signal=False):
            fw.op("pe", lambda h: h.matmul(out, lhsT, rhs, start=start, stop=stop), reads=reads, writes=writes,
                  signal=bool(stop) or len(writes) > 0 or force_signal)

        def rs_from_ss(dst, dstB, src, srcB, addc):
            fw.op("act", lambda h: h.activation(dst, src, AF.Ln, bias=epsb[addc][:, 0:1], scale=1.0),
                  reads=[srcB, constB], writes=[dstB])
            fw.op("act", lambda h: h.activation(dst, dst, AF.Exp, scale=-0.5), reads=[dstB], writes=[dstB])

        citems = []

        def cload(dst, src):
            citems.append((lambda h: h.dma_start(out=dst, in_=src), [], [constB]))
        cload(gA[:], gA_d)
        cload(gM[:], gM_d)
        cload(gq[:], gq_d)
        cload(gk[:], gk_d)
        cload(gs[:], gs_d)
        cload(cw[:], cw_d)
        cload(lamv[:], lam_d)
        cload(crel[:], crel_d)
        cload(maskc[:], maskc_d)
        cload(hflag[:], hflag_d)
        cload(c32[:], ident_d)
        cload(c32b[:], bones_d)
        fw.dma_batch("sp", csem, citems)
        fw.dma_batch("sp", xsem, [(lambda h, c=c: h.dma_start(out=xT[:, c, :], in_=xT_d[c * 128:(c + 1) * 128, :]), [], [xB[c]])
                                  for c in range(NCH)])
        fw.op("dve", lambda h: h.tensor_copy(identb[:], c32[:]), reads=[constB], writes=[constB])
        fw.op("dve", lambda h: h.tensor_copy(bonesb[:], c32b[:]), reads=[constB], writes=[constB])
        fw.op("dve", lambda h: h.memset(onesb[:], 1.0), writes=[constB])
        fw.op("dve", lambda h: h.memset(uhs_bf[:], 0.0), writes=[uhsB])
        for val, tl in epsb.items():
            fw.op("dve", lambda h, val=val, tl=tl: h.memset(tl[:], float(val)), writes=[constB])
        fw.op("dve", lambda h: h.tensor_scalar_mul(gA[:], gA[:], float(math.sqrt(D))), reads=[constB], writes=[constB])
        fw.op("dve", lambda h: h.tensor_scalar_mul(gM[:], gM[:], float(math.sqrt(D))), reads=[constB], writes=[constB])
        fw.op("dve", lambda h: h.tensor_scalar_mul(gk[:], gk[:], 8.0), reads=[constB], writes=[constB])
        for l in range(DEPTH):
            lam_init = 0.8 - 0.6 * math.exp(-0.3 * l)
            fw.op("dve", lambda h, l=l, li=lam_init: h.tensor_scalar_mul(
                gs[:, l:l + 1], gs[:, l:l + 1], float((1.0 - li) * math.sqrt(128.0))), reads=[constB], writes=[constB])
        fw.op("dve", lambda h: h.tensor_tensor(lamp[:, 0], lamv[:, 0], lamv[:, 1], ALU.mult), reads=[constB], writes=[constB])
        fw.op("dve", lambda h: h.tensor_tensor(lamp[:, 1], lamv[:, 2], lamv[:, 3], ALU.mult), reads=[constB], writes=[constB])
        fw.op("dve", lambda h: h.reduce_sum(lams[:], lamp[:], AX.X), reads=[constB], writes=[constB])
        fw.op("act", lambda h: h.activation(lame[:], lams[:], AF.Exp), reads=[constB], writes=[constB])
        fw.op("dve", lambda h: h.tensor_tensor(nlam[:], lame[:, 1], lame[:, 0], ALU.subtract), reads=[constB], writes=[constB])
        for l in range(DEPTH):
            lam_init = 0.8 - 0.6 * math.exp(-0.3 * l)
            fw.op("dve", lambda h, l=l, li=lam_init: h.tensor_scalar_add(nlam[:, l:l + 1], nlam[:, l:l + 1], float(-li)),
                  reads=[constB], writes=[constB])
        fw.op("dve", lambda h: h.tensor_scalar(coth[:], crel[:], maskc[:, 0:1], None, ALU.add), reads=[constB], writes=[constB])
        tstage = R2[:, 0:8192].bitcast(F32)[:, 0:2048].rearrange("p (m q) -> p m q", q=128)
        tdiff = R2[:, 8192:16384].bitcast(F32)[:, 0:2048].rearrange("p (m q) -> p m q", q=128)
        thl = Vt[:, 0:4096].rearrange("p (l m q) -> p l m q", l=2, m=16)
        tstB = Buf("tstage")
        tbd_v = tb_d[:, :].rearrange("p (k l m q) -> p k l m q", k=2, l=2, m=16)
        for kind, src in ((0, t0_d), (1, t1_d)):
            fw.dma("sp", lambda h, src=src: h.dma_start(out=tstage, in_=src), tssem, writes=[tstB])
            for m in range(16):
                fw.op("dve", lambda h, m=m: h.tensor_scalar(tstage[:, m, :], tstage[:, m, :], crel[:, m:m + 1], None, ALU.subtract),
                      reads=[tstB, constB], writes=[tstB])
            fw.op("dve", lambda h: h.tensor_copy(thl[:, 0], tstage), reads=[tstB], writes=[tstB])
            fw.op("dve", lambda h: h.tensor_tensor(tdiff, tstage, thl[:, 0], ALU.subtract), reads=[tstB], writes=[tstB])
            fw.op("dve", lambda h: h.tensor_copy(thl[:, 1], tdiff), reads=[tstB], writes=[tstB])
            fw.dma("sp", lambda h, kind=kind: h.dma_start(out=tbd_v[:, kind], in_=thl), tbsem, reads=[tstB], writes=[tbdB])
        all_r2v = [cmixB[0], cmixB[1], gcuB, accB] + KB + VB + [b for q in QB for b in q] + aGB
        guard([tstB] + all_r2v)

        def rmsnorm_x(gt, l):
            guards = []
            for tt in range(2):
                sl = slice(tt * 512, (tt + 1) * 512)
                pst, pB = ps_all.next()
                for c in range(NCH):
                    sq, sqB = tbfR.next()
                    fw.op("act", lambda h, sq=sq, c=c, sl=sl: h.activation(sq[:], xT[:, c, sl], AF.Square),
                          reads=[xB[c]], writes=[sqB])
                    mm(pst[:], onesb[:], sq[:], c == 0, c == NCH - 1, [sqB, constB], [pB] if c in (0, NCH - 1) else [], force_signal=True)
                rs, rsB = t32R.next()
                rs_from_ss(rs[:], rsB, pst[:], pB, D * EPS)
                for c in range(NCH):
                    fw.op("dve", lambda h, c=c, sl=sl, rs=rs: h.scalar_tensor_tensor(
                        hT[:, c, sl], xT[:, c, sl], gt[:, l, c:c + 1], rs[:], ALU.mult, ALU.mult),
                        reads=[xB[c], rsB, constB] + guards, writes=[hB[tt]])

        def block_fm(nk, rhs_fn, rhs_bufs, evac_fn, ntiles=2):
            wt, wB = w_next()
            pend = []
            for nt in range(ntiles):
                for tt in range(2):
                    pst, pB = ps_all.next()
                    for c in range(nk):
                        mm(pst[:], wt[:, c, nt * 128:(nt + 1) * 128], rhs_fn(c, tt), c == 0, c == nk - 1,
                           [wB[0], wB[1]] + rhs_bufs(c, tt), [pB] if c in (0, nk - 1) else [])
                    if pend:
                        pend.pop()()
                    r = evac_fn(nt, tt, pst, pB)
                    if r is not None:
                        pend.append(r)
            if pend:
                pend.pop()()

        def layer(l):
            guards = []
            rmsnorm_x(gA, l)
            hr = lambda c, tt: hT[:, c, tt * 512:(tt + 1) * 512]
            hb = lambda c, tt: [hB[tt]]

            if stop == 'P1':
                raise _Stop()
            for s in range(4):
                fw.op("dve", lambda h: h.memset(gcu[:, :, 0:2], 0.0), reads=guards, writes=[gcuB])

                def ev_gc(nt, tt, pst, pB):
                    fw.op("act", lambda h: h.activation(gcu[:, nt, 2 + tt * 512: 2 + (tt + 1) * 512], pst[:], AF.Copy),
                          reads=[pB], writes=[gcuB])
                block_fm(16, hr, hb, ev_gc)
                if stop == 'P2a1':
                    raise _Stop()

                def ev_cin(nt, tt, pst, pB):
                    sl = slice(2 + tt * 512, 2 + (tt + 1) * 512)
                    fw.op("dve", lambda h: h.tensor_tensor(gcu[:, nt, sl], pst[:], gcu[:, nt, sl], ALU.mult),
                          reads=[pB, gcuB], writes=[gcuB])
                block_fm(16, hr, hb, ev_cin)
                if stop == 'P2a2':
                    raise _Stop()
                for ci in range(2):
                    ch = 2 * s + ci
                    fw.op("dve", lambda h, ci=ci, ch=ch: h.tensor_scalar(
                        acc[:, ci, :], gcu[:, ci, 2:1026], cw[:, l, 2, ch:ch + 1], None, ALU.mult),
                        reads=[gcuB, constB] + guards, writes=[accB])
                    fw.op("dve", lambda h, ci=ci, ch=ch: h.scalar_tensor_tensor(
                        acc[:, ci, :], gcu[:, ci, 1:1025], cw[:, l, 1, ch:ch + 1], acc[:, ci, :], ALU.mult, ALU.add),
                        reads=[gcuB, accB, constB], writes=[accB])
                    fw.op("dve", lambda h, ci=ci, ch=ch: h.scalar_tensor_tensor(
                        acc[:, ci, :], gcu[:, ci, 0:1024], cw[:, l, 0, ch:ch + 1], acc[:, ci, :], ALU.mult, ALU.add),
                        reads=[gcuB, accB, constB], writes=[accB])
                    fw.op("dve", lambda h, ci=ci, ch=ch: h.tensor_copy(uhs[:, ch, :], gcu[:, ci, 1024:1026]),
                          reads=[gcuB], writes=[uhsB])

                if stop == 'P2a3':
                    raise _Stop()

                def ev_gb(nt, tt, pst, pB):
                    ch = 2 * s + nt
                    if tt == 0 and os.environ.get("DBG_NOGB") != "1":
                        fw.op("act", lambda h: h.activation(gb01[:, ch, :], pst[:, 0:2], AF.Copy), reads=[pB], writes=[gb01B])
                    if os.environ.get("DBG_NOCMIX") == "1":
                        return None
                    fw.op("dve", lambda h: h.tensor_tensor(cmix[:, ch, tt * 512:(tt + 1) * 512], pst[:],
                                                           acc[:, nt, tt * 512:(tt + 1) * 512], ALU.mult),
                          reads=[pB, accB] + ([gb01B] if tt == 0 else []), writes=[cmixB[tt]])
                block_fm(16, hr, hb, ev_gb)
                if stop == 'P2a4':
                    raise _Stop()

            if stop == 'P2a':
                raise _Stop()
            def make_ev_x(jb):
                def ev(nt, tt, pst, pB):
                    c = 2 * jb + nt
                    sl = slice(tt * 512, (tt + 1) * 512)
                    fw.op("dve", lambda h: h.tensor_tensor(xT[:, c, sl], pst[:], xT[:, c, sl], ALU.add),
                          reads=[pB, xB[c]], writes=[xB[c]])
                return ev
            for jb in range(8):
                block_fm(8, lambda c, tt: cmix[:, c, tt * 512:(tt + 1) * 512], lambda c, tt: [cmixB[tt]], make_ev_x(jb))

            if stop == 'P2b':
                raise _Stop()
            def make_ev_qk(dst, dstB_fn, gcol):
                def ev(nt, tt, pst, pB):
                    sq, sqB = tbfR.next()
                    fw.op("act", lambda h: h.activation(sq[:], pst[:], AF.Square), reads=[pB], writes=[sqB])

                    def later():
                        p2, p2B = ps_all.next()
                        mm(p2[:], bonesb[:], sq[:], True, True, [sqB, constB], [p2B])
                        rs, rsB = t32R.next()
                        rs_from_ss(rs[:], rsB, p2[:], p2B, 64 * EPS)
                        fw.op("dve", lambda h: h.scalar_tensor_tensor(
                            dst(nt, tt), pst[:], gcol, rs[:], ALU.mult, ALU.mult),
                            reads=[pB, rsB, constB], writes=dstB_fn(nt, tt))
                    return later
                return ev
            for jb in range(4):
                block_fm(16, hr, hb, make_ev_qk(
                    lambda nt, tt, jb=jb: KT[:, 2 * jb + nt, tt * 512:(tt + 1) * 512],
                    lambda nt, tt, jb=jb: [KB[2 * jb + nt], gcuB], gk[:, l:l + 1]))
            for jb in range(4):
                wt, wB = w_next()
                for j in range(8):
                    pst, pB = ps_all.next()
                    for c in range(NCH):
                        mm(pst[:, 0:WCOLS], hT[:, c, j * 128:(j + 1) * 128], wt[:, c, :], c == 0, c == NCH - 1,
                           [wB[0], wB[1], hB[j // 4]], [pB] if c in (0, NCH - 1) else [])
                    fw.op("act", lambda h, j=j, jb=jb, pst=pst: h.activation(
                        Vv[:, 2 * jb:2 * jb + 2, j, :], pst[:, 0:WCOLS].rearrange("p (h d) -> p h d", h=2), AF.Copy),
                        reads=[pB], writes=[VB[2 * jb], VB[2 * jb + 1], accB])
            if stop == 'P2c':
                raise _Stop()
            uhs_f = uhs[:, :, :].rearrange("p c t -> p (c t)")
            fw.op("dve", lambda h: h.tensor_copy(uhs_bf[:, 0:16], uhs_f), reads=[uhsB], writes=[uhsB])
            fw.op("dve", lambda h: h.tensor_tensor(uhs_t[:], uhs_f, uhs_bf[:, 0:16], ALU.subtract), reads=[uhsB], writes=[uhsB])
            fw.op("dve", lambda h: h.tensor_copy(uhs_bf[:, 16:32], uhs_t[:]), reads=[uhsB], writes=[uhsB])
            groups = [[0, 1], [2, 3], [4, 5], [6, 7]]
            fw.dma("sp", lambda h: h.dma_start(out=kvinK[:, :].rearrange("p (h x) -> p h x", h=8), in_=KT), kvsemK, reads=KB, writes=[kvinKB])
            fw.dma("pool", lambda h: h.collective_compute("AllGather", ALU.bypass, replica_groups=groups,
                                                          ins=[kvinK.ap().opt()], outs=[kvoutK.ap().opt()]),
                   ccsemK, reads=[kvinKB], writes=[kvoutKB], inc=1)
            fw.dma_batch("sp", kvsemV, [
                (lambda h: h.dma_start(out=kvinV[:, :], in_=Vt[:, :]), VB, [kvinVB]),
                (lambda h: h.dma_start(out=kvinH[:, :], in_=uhs_bf[:, :]), [uhsB], [kvinHB]),
            ])
            fw.dma("pool", lambda h: h.collective_compute("AllGather", ALU.bypass, replica_groups=groups,
                                                          ins=[kvinV.ap().opt()], outs=[kvoutV.ap().opt()]),
                   ccsemV, reads=[kvinVB], writes=[kvoutVB], inc=1)
            fw.dma("pool", lambda h: h.collective_compute("AllGather", ALU.bypass, replica_groups=groups,
                                                          ins=[kvinH.ap().opt()], outs=[kvoutH.ap().opt()]),
                   ccsemH, reads=[kvinHB], writes=[kvoutHB], inc=1)
            if stop == 'EX':
                raise _Stop()
            for jb in range(4):
                block_fm(16, hr, hb, make_ev_qk(
                    lambda nt, tt, jb=jb: QT[:, 2 * jb + nt, tt * 512:(tt + 1) * 512],
                    lambda nt, tt, jb=jb: [QB[2 * jb + nt][tt], cmixB[tt]], gq[:, l:l + 1]))

            if stop == 'P2d':
                raise _Stop()
            fw.dma("sp", lambda h: h.dma_start(out=R1[:, 0:8192], in_=tb_d[:, :]), tbsem, reads=[tbdB], writes=[tbB, hB[0], hB[1]])
            fw.dma("sp", lambda h: h.dma_start(out=uhr_bf[:, :, :].rearrange("p a b -> p (a b)"), in_=kvoutH[0:128, 0:32]),
                   uhsem, reads=[kvoutHB], writes=[uhrB])
            fw.op("dve", lambda h: h.tensor_tensor(uhr[:, :, :].rearrange("p c t -> p (c t)"), uhr_bf[:, 0, :], uhr_bf[:, 1, :], ALU.add),
                  reads=[uhrB], writes=[uhrB])
            O1, O1B = ps[4], psB[4]
            O2, O2B = ps[5], psB[5]
            S1, S1B = ps[6], psB[6]
            S2, S2B = ps[7], psB[7]
            Oacc = [(O1, O1B, S1, S1B), (O2, O2B, S2, S2B)]

            def load_other(hh):
                i = hh % 2
                fw.dma_batch("sp", kvosem[i], [
                    (lambda h: h.dma_start(out=kvo[i][:, 0:1024], in_=kvoutK[0:128, hh * 1024:(hh + 1) * 1024]), [kvoutKB], [kvoBuf[i]]),
                    (lambda h: h.dma_start(out=kvo[i][:, 1024:2048], in_=kvoutV[0:128, hh * 1024:(hh + 1) * 1024]), [kvoutVB], [kvoBuf[i]]),
                ])
            kvoBuf = kvoB
            load_other(0)
            for hh in range(8):
                if hh + 1 < 8:
                    load_other(hh + 1)
                ko = kvo[hh % 2]
                koB = kvoB[hh % 2]
                for I in range(2):
                    qlo = I * 512
                    steps = [(False, j) for j in range(4 * I + 4)] + [(True, j) for j in range(8)]
                    nsteps = len(steps)
                    pend = None
                    for si, (oth, j) in enumerate(steps):
                        if oth:
                            q0 = qlo
                        else:
                            q0 = max(qlo, j * 128)
                        N = qlo + 512 - q0
                        c0 = q0 - qlo
                        Ptiles = []
                        for m in range(2):
                            hm = 2 * hh + m
                            pst, pB = ps_lo.next()
                            prow = slice(m * 64, (m + 1) * 64)
                            if oth:
                                lhsT = ko[prow, j * 128:(j + 1) * 128]
                                kb = koB
                            else:
                                lhsT = KT[prow, hh, j * 128:(j + 1) * 128]
                                kb = KB[hh]
                            tl = []
                            if oth:
                                if j == 7 and I == 0:
                                    tl.append((1, 0))
                            else:
                                if j * 128 >= qlo:
                                    tl.append((0, 0))
                                    if (j + 1) * 128 < qlo + 512:
                                        tl.append((1, 128))
                                elif (j + 1) * 128 == qlo:
                                    tl.append((1, 0))
                            mm(pst[:, 0:N], lhsT, QT[prow, hh, q0:q0 + N], True, len(tl) == 0,
                               [kb, QB[hh][I]], [pB])
                            for ti, (kind, off) in enumerate(tl):
                                for hl in range(2):
                                    last = (ti == len(tl) - 1) and hl == 1
                                    mm(pst[:, off:off + 128], identb[:], tbv[:, kind, hl, hm, :], False, last,
                                       [tbB, constB], [pB] if last else [])
                            pt, ptB = PtR.next()
                            bias_ap = coth[:, hm:hm + 1] if oth else crel[:, hm:hm + 1]
                            fw.op("act", lambda h, pt=pt, pst=pst, N=N, bias_ap=bias_ap: h.activation(
                                pt[:, 0:N], pst[:, 0:N], AF.Exp, bias=bias_ap, scale=1.0),
                                reads=[pB, constB], writes=[ptB])
                            Ptiles.append((pt, ptB, N, c0))
                        if pend is not None:
                            pend()

                        def pv(si=si, oth=oth, j=j, Ptiles=Ptiles):
                            first = si == 0
                            last = si == nsteps - 1
                            for m in range(2):
                                pt, ptB, N, c0 = Ptiles[m]
                                Oa, OaB, Sa, SaB = Oacc[m]
                                if oth:
                                    vl = ko[:, 1024 + j * 128: 1024 + (j + 1) * 128]
                                    vb = koB
                                else:
                                    vl = Vv[:, hh, j, :]
                                    vb = VB[hh]
                                mm(Oa[:, c0:c0 + N], vl, pt[:, 0:N], first, last, [ptB, vb], [OaB] if (first or last) else [])
                                mm(Sa[:, c0:c0 + N], onesb[:], pt[:, 0:N], first, last, [ptB, constB], [SaB] if (first or last) else [])
                        pend = pv
                    pend()
                    r1, r1B = t32R.next()
                    r2, r2B = t32R.next()
                    l1, l1B = t32R.next()
                    l2, l2B = t32R.next()
                    fw.op("act", lambda h, l1=l1: h.activation(l1[:], S1[:], AF.Ln), reads=[S1B], writes=[l1B])
                    fw.op("dve", lambda h, r1=r1: h.tensor_copy(r1[:], O1[:]), reads=[O1B], writes=[r1B])
                    fw.op("act", lambda h, l2=l2: h.activation(l2[:], S2[:], AF.Ln), reads=[S2B], writes=[l2B])
                    fw.op("dve", lambda h, r2=r2: h.tensor_copy(r2[:], O2[:]), reads=[O2B], writes=[r2B])
                    fw.op("act", lambda h, l1=l1: h.activation(l1[:], l1[:], AF.Exp, scale=-1.0), reads=[l1B], writes=[l1B])
                    fw.op("act", lambda h, l2=l2: h.activation(l2[:], l2[:], AF.Exp, scale=-1.0), reads=[l2B], writes=[l2B])
                    fw.op("dve", lambda h, r1=r1, l1=l1: h.tensor_tensor(r1[:], r1[:], l1[:], ALU.mult), reads=[r1B, l1B], writes=[r1B])
                    fw.op("dve", lambda h, r2=r2, l2=l2: h.tensor_tensor(r2[:], r2[:], l2[:], ALU.mult), reads=[r2B, l2B], writes=[r2B])
                    fw.op("dve", lambda h, r1=r1, r2=r2: h.scalar_tensor_tensor(
                        r1[:], r2[:], nlam[:, l:l + 1], r1[:], ALU.mult, ALU.add), reads=[r1B, r2B, constB], writes=[r1B])
                    sq, sqB = tbfR.next()
                    fw.op("dve", lambda h, r1=r1, sq=sq: h.tensor_tensor(sq[:], r1[:], r1[:], ALU.mult), reads=[r1B], writes=[sqB])
                    p2, p2B = ps_lo.next()
                    mm(p2[:], onesb[:], sq[:], True, True, [sqB, constB], [p2B])
                    rs_from_ss(r2[:], r2B, p2[:], p2B, 128 * EPS)
                    fw.op("dve", lambda h, r1=r1, r2=r2, hh=hh, I=I: h.scalar_tensor_tensor(
                        QT[:, hh, I * 512:(I + 1) * 512], r1[:], gs[:, l:l + 1], r2[:], ALU.mult, ALU.mult),
                        reads=[r1B, r2B, constB], writes=[QB[hh][I]])

            if stop == 'P3':
                raise _Stop()
            for jb in range(8):
                block_fm(8, lambda c, tt: QT[:, c, tt * 512:(tt + 1) * 512], lambda c, tt: [QB[c][tt]], make_ev_x(jb))
            fw.op("dve", lambda h: h.tensor_tensor(dtmp[:, :, 0], uhr[:, :, 0], cw[:, l, 0, :], ALU.mult), reads=[uhrB, constB], writes=[dtmpB])
            fw.op("dve", lambda h: h.tensor_tensor(dtmp2[:, :, 0], uhr[:, :, 1], cw[:, l, 1, :], ALU.mult), reads=[uhrB, constB, dtmpB], writes=[dtmpB])
            fw.op("dve", lambda h: h.tensor_tensor(dtmp[:, :, 0], dtmp[:, :, 0], dtmp2[:, :, 0], ALU.add), reads=[dtmpB], writes=[dtmpB])
            fw.op("dve", lambda h: h.tensor_tensor(dtmp[:, :, 1], uhr[:, :, 1], cw[:, l, 0, :], ALU.mult), reads=[uhrB, constB, dtmpB], writes=[dtmpB])
            fw.op("dve", lambda h: h.tensor_tensor(dtmp[:, :, :], dtmp[:, :, :], gb01[:, :, :], ALU.mult), reads=[dtmpB, gb01B], writes=[dtmpB])
            fw.op("dve", lambda h: h.tensor_scalar(dmix[:, :, :], dtmp[:, :, :], hflag[:, 0:1], None, ALU.mult), reads=[dtmpB, constB], writes=[dmixB])
            for jb in range(8):
                wt, wB = w_next()
                for nt in range(2):
                    c = 2 * jb + nt
                    pst, pB = ps_all.next()
                    for k in range(8):
                        mm(pst[:, 0:2], wt[:, k, nt * 128:(nt + 1) * 128], dmix[:, k, :], k == 0, k == 7,
                           [wB[0], wB[1], dmixB], [pB] if k in (0, 7) else [])
                    fw.op("dve", lambda h, c=c, pst=pst: h.tensor_tensor(xT[:, c, 0:2], pst[:, 0:2], xT[:, c, 0:2], ALU.add),
                          reads=[pB, xB[c]], writes=[xB[c]])

            if stop == 'P4':
                raise _Stop()
            guard([tbB] + PtB + kvoB + [hB[0], hB[1]])
            rmsnorm_x(gM, l)
            guard(all_r2v)
            for g in range(4):
                def make_ev_up(jb):
                    def ev(nt, tt, pst, pB):
                        cc = 2 * jb + nt
                        r, rB = tbfR.next()
                        fw.op("act", lambda h: h.activation(r[:], pst[:], AF.Relu), reads=[pB], writes=[rB])
                        fw.op("dve", lambda h: h.tensor_tensor(aG[:, cc, tt * 512:(tt + 1) * 512], r[:], r[:], ALU.mult),
                              reads=[rB], writes=[aGB[tt]])
                    return ev
                for jb in range(8):
                    block_fm(16, hr, hb, make_ev_up(jb))
                for jb in range(8):
                    block_fm(16, lambda c, tt: aG[:, c, tt * 512:(tt + 1) * 512], lambda c, tt: [aGB[tt]], make_ev_x(jb))
            guard(all_r2v)

        try:
            for l in layers:
                layer(l)
        except _Stop:
            pass

        outB = Buf("out")
        fw.dma_batch("sp", osem, [(lambda h, c=c: h.dma_start(out=yT_d[c * 128:(c + 1) * 128, :], in_=xT[:, c, :]), [xB[c]], [outB])
                                  for c in range(NCH)])
        fw.wait_all("sp", [outB])
        fw.drain("sp")
        fw.finish()
    return nc


def _bucket_table():
    n = np.arange(256)
    max_exact, nb, maxd = 16, 32, 128
    nf = np.maximum(n, max_exact).astype(np.float32)
    large = max_exact + (np.log(nf / max_exact) / math.log(maxd / max_exact) * (nb - max_exact)).astype(np.int32)
    large = np.minimum(large, nb - 1)
    return np.where(n < max_exact, n, large)


def _prep_common(inp):
    f = np.float32
    c = {}
    for k in ("w_in", "w_out", "w_up", "w_down"):
        for l in range(DEPTH):
            c[f"{k}{l}"] = np.ascontiguousarray(inp[k][l], dtype=f)
    c["gA"] = np.ascontiguousarray(inp["attn_norm_g"].reshape(DEPTH, NCH, 128).transpose(2, 0, 1), dtype=f)
    c["gM"] = np.ascontiguousarray(inp["mlp_norm_g"].reshape(DEPTH, NCH, 128).transpose(2, 0, 1), dtype=f)
    c["gq"] = np.ascontiguousarray(np.tile(inp["q_norm_g"].T, (2, 1)), dtype=f)
    c["gk"] = np.ascontiguousarray(np.tile(inp["k_norm_g"].T, (2, 1)), dtype=f)
    c["gs"] = np.ascontiguousarray(inp["subln_g"].T, dtype=f)
    c["cw"] = np.ascontiguousarray(inp["conv_w"].reshape(DEPTH, 3, 8, 128).transpose(3, 0, 1, 2), dtype=f)
    lam = np.stack([inp["lambda_q1"], inp["lambda_k1"], inp["lambda_q2"], inp["lambda_k2"]], axis=0)
    c["lamv"] = np.ascontiguousarray(np.broadcast_to(lam[None], (128, 4, DEPTH, 64)), dtype=f)
    rb = np.asarray(inp["rel_bias"], dtype=f)
    c["crel"] = np.ascontiguousarray(np.broadcast_to(rb[31][None, :], (128, 16)), dtype=f)
    bt = _bucket_table()
    kk = np.arange(128)[:, None]
    qq = np.arange(128)[None, :]
    d0 = qq - kk
    g0 = rb[bt[np.clip(d0, 0, 255)]]
    t0 = np.where((d0 >= 0)[:, :, None], g0, f(NEG))
    t1 = rb[bt[np.clip(128 + d0, 0, 255)]]
    c["t0"] = np.ascontiguousarray(t0.transpose(0, 2, 1), dtype=f)
    c["t1"] = np.ascontiguousarray(t1.transpose(0, 2, 1), dtype=f)
    c["ident"] = np.eye(128, dtype=f)
    bo = np.zeros((128, 128), dtype=f)
    bo[:64, :64] = 1.0
    bo[64:, 64:] = 1.0
    c["bones"] = bo
    return c


def _run(nc, common, xTs, n_cores=8, layers=tuple(range(DEPTH))):
    in_maps = []
    wkeys = [f"{k}{l}" for k in ("w_in", "w_out", "w_up", "w_down") for l in range(DEPTH)]
    for core in range(n_cores):
        half = core % 2
        m = {k: v for k, v in common.items() if k not in wkeys}
        for k in ("w_in", "w_out", "w_up", "w_down"):
            for l in layers:
                m[f"{k}{l}"] = common[f"{k}{l}"]
        m["xT"] = xTs[core]
        m["maskc"] = np.full((128, 1), 0.0 if half == 1 else NEG, dtype=np.float32)
        m["hflag"] = np.full((128, 1), 1.0 if half == 1 else 0.0, dtype=np.float32)
        in_maps.append(m)
    res = run_bass_kernel_spmd(nc, in_maps, core_ids=list(range(n_cores)))
    return [r["yT"] for r in res.results]


_PROG_CACHE = {}


def kernel(**inputs):
    inp = {k: np.asarray(v) for k, v in inputs.items()}
    x = np.asarray(inp["x"], dtype=np.float32)
    common = _prep_common(inp)
    xTs = []
    for core in range(8):
        b, half = core // 2, core % 2
        xTs.append(np.ascontiguousarray(x[b, half * T:(half + 1) * T, :].T))
    key = tuple(range(DEPTH))
    if key not in _PROG_CACHE:
        _PROG_CACHE[key] = build_program(list(range(DEPTH)))
    outs = _run(_PROG_CACHE[key], common, xTs)
    y = np.empty((NB, S, D), dtype=np.float32)
    for core in range(8):
        b, half = core // 2, core % 2
        y[b, half * T:(half + 1) * T, :] = outs[core].T
    return y
```
